# Optimizing a Trainium2 kernel written in Bass

```python
import math
import jax, jax.numpy as jnp
from jax import lax
import numpy as np

D_MODEL = 2048
BATCH = 2
SEQ = 16384
DEPTH = 2
DEC_BATCH = 4
DEC_SEQ = 4096
PAST_LEN = 128

MLA_HEADS = 6
MLA_Q_RANK = 512
MLA_KV_RANK = 256
MLA_NOPE = 128
MLA_ROPE = 64
MLA_V = 128
DIFF_HEADS = 6
DIFF_DK = 64
DIFF_DV = 2 * DIFF_DK
CONV_CH = 512
CONV_WIDTH = 31
MIX_WIDTH = MLA_HEADS * MLA_V + DIFF_HEADS * DIFF_DV + CONV_CH
IN_SIZES = (MLA_Q_RANK, MLA_KV_RANK, MLA_ROPE,
            DIFF_HEADS * 2 * DIFF_DK, DIFF_HEADS * 2 * DIFF_DK, DIFF_HEADS * DIFF_DV,
            2 * CONV_CH)
IN_COLS = sum(IN_SIZES)
D_FF = 5632
N_MOD = 9
ROPE_THETA = 10000.0
Q_BLOCK = 128
EPS = 1e-6

kernel_name = "hybrid_mla_diff_conformer_encoder"


def rms_norm(x, g):
    xf = x.astype(jnp.float32)
    y = xf * lax.rsqrt(jnp.mean(xf * xf, axis=-1, keepdims=True) + EPS)
    return (y * g.astype(jnp.float32)).astype(x.dtype)


def layer_norm(x, g, b):
    xf = x.astype(jnp.float32)
    mu = jnp.mean(xf, axis=-1, keepdims=True)
    var = jnp.mean(jnp.square(xf - mu), axis=-1, keepdims=True)
    y = (xf - mu) * lax.rsqrt(var + EPS)
    return (y * g.astype(jnp.float32) + b.astype(jnp.float32)).astype(x.dtype)


def rotary(x, pos):
    d = x.shape[-1]
    half = d // 2
    inv_freq = jnp.float32(ROPE_THETA) ** (-jnp.arange(half, dtype=jnp.float32) * (2.0 / d))
    ang = pos.astype(jnp.float32)[:, None] * inv_freq[None, :]
    bshape = (1, pos.shape[0]) + (1,) * (x.ndim - 3) + (half,)
    cos = jnp.cos(ang).reshape(bshape)
    sin = jnp.sin(ang).reshape(bshape)
    xf = x.astype(jnp.float32)
    x1, x2 = xf[..., :half], xf[..., half:]
    return jnp.concatenate([x1 * cos - x2 * sin, x1 * sin + x2 * cos], axis=-1).astype(x.dtype)


def block_attention(q, k, v, map_w, scale):
    B, S, H, M, dk = q.shape
    dv = v.shape[-1]
    nb = S // Q_BLOCK
    qb = jnp.moveaxis(q.reshape(B, nb, Q_BLOCK, H, M, dk), 1, 0)

    def one_block(qi):
        s = jnp.einsum('bqhmd,bkhmd->bhmqk', qi, k, preferred_element_type=jnp.float32) * scale
        p = jax.nn.softmax(s, axis=-1)
        p = jnp.einsum('bhmqk,m->bhqk', p, map_w)
        return jnp.einsum('bhqk,bkhd->bqhd', p.astype(v.dtype), v)

    o = lax.map(one_block, qb)
    return jnp.moveaxis(o, 0, 1).reshape(B, S, H, dv)


def swiglu(h, w_up, w_down):
    a, b = jnp.split(h @ w_up, 2, axis=-1)
    return (jax.nn.silu(a) * b) @ w_down


def token_mixer(h, pos, layer_idx, w_in, g_cq, g_ckv, w_uq, w_ukv, g_mla_q, g_mla_k,
                g_diff_q, g_diff_k, lam_q1, lam_k1, lam_q2, lam_k2, g_diff_sub,
                w_dw, b_dw, g_conv, b_conv, w_out):
    B, S, _ = h.shape
    split_at = [int(v) for v in np.cumsum(IN_SIZES)[:-1]]
    cq, ckv, k_rope, dq, dk, dv, cu = jnp.split(h @ w_in, split_at, axis=-1)

    q = (rms_norm(cq, g_cq) @ w_uq).reshape(B, S, MLA_HEADS, MLA_NOPE + MLA_ROPE)
    kv = (rms_norm(ckv, g_ckv) @ w_ukv).reshape(B, S, MLA_HEADS, MLA_NOPE + MLA_V)
    k_nope, v = kv[..., :MLA_NOPE], kv[..., MLA_NOPE:]
    k_r = jnp.broadcast_to(k_rope[:, :, None, :], (B, S, MLA_HEADS, MLA_ROPE))
    k = jnp.concatenate([k_nope, k_r], axis=-1)
    q = rms_norm(q, g_mla_q)
    k = rms_norm(k, g_mla_k)
    q = jnp.concatenate([q[..., :MLA_NOPE], rotary(q[..., MLA_NOPE:], pos)], axis=-1)
    k = jnp.concatenate([k[..., :MLA_NOPE], rotary(k[..., MLA_NOPE:], pos)], axis=-1)
    o_mla = block_attention(q[:, :, :, None], k[:, :, :, None], v,
                            jnp.ones((1,), jnp.float32), 1.0 / math.sqrt(MLA_NOPE + MLA_ROPE))

    lam_init = 0.8 - 0.6 * math.exp(-0.3 * layer_idx)
    lam = (jnp.exp(jnp.sum(lam_q1.astype(jnp.float32) * lam_k1.astype(jnp.float32)))
           - jnp.exp(jnp.sum(lam_q2.astype(jnp.float32) * lam_k2.astype(jnp.float32)))
           + lam_init)
    dq = rotary(rms_norm(dq.reshape(B, S, DIFF_HEADS, 2, DIFF_DK), g_diff_q), pos)
    dk = rotary(rms_norm(dk.reshape(B, S, DIFF_HEADS, 2, DIFF_DK), g_diff_k), pos)
    dv = dv.reshape(B, S, DIFF_HEADS, DIFF_DV)
    map_w = jnp.stack([jnp.ones((), jnp.float32), -lam])
    o_diff = block_attention(dq, dk, dv, map_w, 1.0 / math.sqrt(DIFF_DK))
    o_diff = rms_norm(o_diff, g_diff_sub) * (1.0 - lam_init)

    a, gt = jnp.split(cu, 2, axis=-1)
    glu = a * jax.nn.sigmoid(gt)
    conv = lax.conv_general_dilated(
        glu, w_dw[:, None, :].astype(glu.dtype), window_strides=(1,),
        padding=[(CONV_WIDTH // 2, CONV_WIDTH // 2)],
        dimension_numbers=('NWC', 'WIO', 'NWC'), feature_group_count=CONV_CH)
    conv = conv + b_dw.astype(conv.dtype)
    o_conv = jax.nn.silu(layer_norm(conv, g_conv, b_conv))

    mix = jnp.concatenate([o_mla.reshape(B, S, MLA_HEADS * MLA_V),
                           o_diff.reshape(B, S, DIFF_HEADS * DIFF_DV),
                           o_conv], axis=-1)
    return mix @ w_out


def encoder_layer(x, c, pos, layer_idx, w_ada, b_ada, g_norm, w_ffn1_in, w_ffn1_out,
                  w_ffn2_in, w_ffn2_out, w_in, g_cq, g_ckv, w_uq, w_ukv, g_mla_q, g_mla_k,
                  g_diff_q, g_diff_k, lam_q1, lam_k1, lam_q2, lam_k2, g_diff_sub,
                  w_dw, b_dw, g_conv, b_conv, w_out):
    B = x.shape[0]
    mod = (jax.nn.silu(c) @ w_ada + b_ada).reshape(B, N_MOD, D_MODEL)
    sh1, sc1, ga1, sh2, sc2, ga2, sh3, sc3, ga3 = [mod[:, i, None, :] for i in range(N_MOD)]

    h = rms_norm(x, g_norm[0]) * (1.0 + sc1) + sh1
    x = x + 0.5 * ga1 * swiglu(h, w_ffn1_in, w_ffn1_out)

    h = rms_norm(x, g_norm[1]) * (1.0 + sc2) + sh2
    x = x + ga2 * token_mixer(h, pos, layer_idx, w_in, g_cq, g_ckv, w_uq, w_ukv, g_mla_q, g_mla_k,
                              g_diff_q, g_diff_k, lam_q1, lam_k1, lam_q2, lam_k2, g_diff_sub,
                              w_dw, b_dw, g_conv, b_conv, w_out)

    h = rms_norm(x, g_norm[2]) * (1.0 + sc3) + sh3
    x = x + 0.5 * ga3 * swiglu(h, w_ffn2_in, w_ffn2_out)
    return rms_norm(x, g_norm[3])


def setup_inputs(seed: int = 0) -> dict:
    key = jax.random.key(seed)
    ks = jax.random.split(key, 32)
    f32 = jnp.float32

    def nrm(k, shape, scale):
        return jax.random.normal(k, shape, f32) * scale

    def gain(k, shape):
        return 1.0 + 0.05 * jax.random.normal(k, shape, f32)

    L = DEPTH
    return {
        "x_prompt": nrm(ks[0], (BATCH, SEQ, D_MODEL), 1.0),
        "x_sample": nrm(ks[1], (DEC_BATCH, DEC_SEQ, D_MODEL), 1.0),
        "c_prompt": nrm(ks[2], (BATCH, D_MODEL), 1.0),
        "c_sample": nrm(ks[3], (DEC_BATCH, D_MODEL), 1.0),
        "w_ada": nrm(ks[4], (L, D_MODEL, N_MOD * D_MODEL), 0.5 * D_MODEL ** -0.5),
        "b_ada": nrm(ks[5], (L, N_MOD * D_MODEL), 0.02),
        "g_norm": gain(ks[6], (L, 4, D_MODEL)),
        "w_ffn1_in": nrm(ks[7], (L, D_MODEL, 2 * D_FF), D_MODEL ** -0.5),
        "w_ffn1_out": nrm(ks[8], (L, D_FF, D_MODEL), D_FF ** -0.5),
        "w_ffn2_in": nrm(ks[9], (L, D_MODEL, 2 * D_FF), D_MODEL ** -0.5),
        "w_ffn2_out": nrm(ks[10], (L, D_FF, D_MODEL), D_FF ** -0.5),
        "w_in": nrm(ks[11], (L, D_MODEL, IN_COLS), D_MODEL ** -0.5),
        "g_cq": gain(ks[12], (L, MLA_Q_RANK)),
        "g_ckv": gain(ks[13], (L, MLA_KV_RANK)),
        "w_uq": nrm(ks[14], (L, MLA_Q_RANK, MLA_HEADS * (MLA_NOPE + MLA_ROPE)), MLA_Q_RANK ** -0.5),
        "w_ukv": nrm(ks[15], (L, MLA_KV_RANK, MLA_HEADS * (MLA_NOPE + MLA_V)), MLA_KV_RANK ** -0.5),
        "g_mla_q": gain(ks[16], (L, MLA_NOPE + MLA_ROPE)),
        "g_mla_k": gain(ks[17], (L, MLA_NOPE + MLA_ROPE)),
        "g_diff_q": gain(ks[18], (L, DIFF_DK)),
        "g_diff_k": gain(ks[19], (L, DIFF_DK)),
        "lam_q1": nrm(ks[20], (L, DIFF_DK), 0.1),
        "lam_k1": nrm(ks[21], (L, DIFF_DK), 0.1),
        "lam_q2": nrm(ks[22], (L, DIFF_DK), 0.1),
        "lam_k2": nrm(ks[23], (L, DIFF_DK), 0.1),
        "g_diff_sub": gain(ks[24], (L, DIFF_DV)),
        "w_dw": nrm(ks[25], (L, CONV_WIDTH, CONV_CH), CONV_WIDTH ** -0.5),
        "b_dw": nrm(ks[26], (L, CONV_CH), 0.02),
        "g_conv": gain(ks[27], (L, CONV_CH)),
        "b_conv": nrm(ks[28], (L, CONV_CH), 0.02),
        "w_out": nrm(ks[29], (L, MIX_WIDTH, D_MODEL), MIX_WIDTH ** -0.5),
    }


def reference(x_prompt, x_sample, c_prompt, c_sample, w_ada, b_ada, g_norm, w_ffn1_in, w_ffn1_out,
              w_ffn2_in, w_ffn2_out, w_in, g_cq, g_ckv, w_uq, w_ukv, g_mla_q, g_mla_k,
              g_diff_q, g_diff_k, lam_q1, lam_k1, lam_q2, lam_k2, g_diff_sub,
              w_dw, b_dw, g_conv, b_conv, w_out):
    def trunk(x, c):
        pos = jnp.arange(x.shape[1], dtype=jnp.int32)
        for l in range(DEPTH):
            x = encoder_layer(x, c, pos, l, w_ada[l], b_ada[l], g_norm[l], w_ffn1_in[l], w_ffn1_out[l],
                              w_ffn2_in[l], w_ffn2_out[l], w_in[l], g_cq[l], g_ckv[l], w_uq[l], w_ukv[l],
                              g_mla_q[l], g_mla_k[l], g_diff_q[l], g_diff_k[l],
                              lam_q1[l], lam_k1[l], lam_q2[l], lam_k2[l], g_diff_sub[l],
                              w_dw[l], b_dw[l], g_conv[l], b_conv[l], w_out[l])
        return x

    y_prompt = trunk(x_prompt, c_prompt)
    y_sample = trunk(x_sample, c_sample)
    return (y_prompt, y_sample)
```

```python
import contextlib
import math
import numpy as np
import concourse.bass as bass
import concourse.mybir as mybir
from concourse.bass_utils import run_bass_kernel_spmd

F32 = mybir.dt.float32
BF16 = mybir.dt.bfloat16
AF = mybir.ActivationFunctionType
ALU = mybir.AluOpType

NCORES = 8
CFG = {"L": 2, "tiles": list(range(12))}
D = 2048
KD = 16
DFF = 5632
NJ = 44
TT = 512
NT = 12
NTOK = NT * TT
SEQ_OF_TILE = [0, 0, 0, 0, 1, 1, 1, 1, 2, 3, 4, 5]
NSEQ = 6
EPS = 1e-6
NIMG = 30
NQIMG = 18
SLAB_PER_LAYER = 400
SLAB_PER_RANK = 50
OC_PER_RANK = 18


class Stream:
    def __init__(s, name, sem, unit):
        s.name = name; s.sem = sem; s.unit = unit; s.cnt = 0; s.clocks = [None]


class Eng:
    def __init__(s, name, stream, self_sync):
        s.name = name; s.stream = stream; s.known = {}; s.self_sync = self_sync; s.prog = []


class T:
    __slots__ = ("name", "w", "r", "excl")

    def __init__(s, name, excl=False):
        s.name = name; s.w = None; s.r = []; s.excl = excl


class Sched:
    def __init__(s, nc, es):
        s.nc = nc; s.es = es; s.E = {}; s.dry = False
        for name, ss in (("pe", False), ("act", True), ("dve", True), ("pool", True), ("sp", False)):
            s.E[name] = Eng(name, Stream(name, s.newsem(name), 1), ss)

    def newsem(s, name):
        return s.es.enter_context(s.nc.semaphore("sem_" + name))

    def slot(s, name, unit=16):
        return Stream(name, s.newsem(name), unit)

    def _wait(s, eng, dep):
        st, n = dep
        if st is eng.stream and not eng.self_sync:
            return
        if eng.known.get(st.name, 0) >= n:
            return
        eng.prog.append((None, st.sem, n * st.unit))
        kn = eng.known
        for k, v in st.clocks[n].items():
            if kn.get(k, 0) < v:
                kn[k] = v

    def _deps(s, eng, reads, writes):
        for t in reads:
            if t.w is not None:
                s._wait(eng, t.w)
            if t.excl:
                for d in t.r:
                    if d[0] is not eng.stream:
                        s._wait(eng, d)
        for t in writes:
            if t.w is not None:
                s._wait(eng, t.w)
            for d in t.r:
                s._wait(eng, d)

    def _commit(s, st, clk, reads, writes):
        st.cnt += 1
        n = st.cnt
        c = dict(clk); c[st.name] = n
        st.clocks.append(c)
        for t in writes:
            t.w = (st, n); t.r = []
        for t in reads:
            t.r.append((st, n))
            if len(t.r) > 16:
                best = {}
                for (a, b) in t.r:
                    if a.name not in best or best[a.name][1] < b:
                        best[a.name] = (a, b)
                t.r = list(best.values())

    def op(s, en, fn, reads=(), writes=()):
        if s.dry:
            return
        eng = s.E[en]
        s._deps(eng, reads, writes)
        eng.prog.append((fn, eng.stream.sem, 1))
        s._commit(eng.stream, eng.known, reads, writes)

    def dma(s, qn, slot, out, in_, reads=(), writes=()):
        if s.dry:
            return
        s.custom(qn, slot, lambda e: e.dma_start(out=out, in_=in_), reads, writes)

    def custom(s, qn, slot, fn, reads=(), writes=()):
        if s.dry:
            return
        eng = s.E[qn]
        s._deps(eng, reads, writes)
        if slot.cnt > 0:
            s._wait(eng, (slot, slot.cnt))
        eng.prog.append((fn, slot.sem, slot.unit))
        s._commit(slot, eng.known, reads, writes)

    def finish(s, en, tiles):
        eng = s.E[en]
        for t in tiles:
            if t.w is not None:
                s._wait(eng, t.w)

    def emit(s, blk):
        def run(prog):
            def body(e):
                for fn, sem, v in prog:
                    if fn is None:
                        e.wait_ge(sem, v)
                    else:
                        fn(e).then_inc(sem, v)
            return body
        blk.tensor(run(s.E["pe"].prog))
        blk.scalar(run(s.E["act"].prog))
        blk.vector(run(s.E["dve"].prog))
        blk.gpsimd(run(s.E["pool"].prog))
        blk.sync(run(s.E["sp"].prog))


def slab_plan():
    P = []
    ar = np.arange

    def ffn(name_in, name_out):
        for j in range(NJ):
            P.append((name_in, list(range(KD)), ar(j * 128, (j + 1) * 128)))
            P.append((name_in, list(range(KD)), ar(DFF + j * 128, DFF + (j + 1) * 128)))
        for m in range(KD):
            for q in range(4):
                P.append((name_out, list(range(q * 11, (q + 1) * 11)), ar(m * 128, (m + 1) * 128)))

    def perm64(cols):
        c = np.asarray(cols).reshape(-1, 64)
        return np.concatenate([np.concatenate([b[32:], b[:32]]) for b in c])

    ffn("w_ffn1_in", "w_ffn1_out")
    k16 = list(range(KD))
    for i in range(4):
        P.append(("w_in", k16, ar(i * 128, (i + 1) * 128)))
    for i in range(2):
        P.append(("w_in", k16, ar(512 + i * 128, 512 + (i + 1) * 128)))
    P.append(("w_in", k16, ar(768, 832)))
    P.append(("w_in", k16, perm64(ar(768, 832))))
    for base in (832, 1600):
        for h in range(6):
            P.append(("w_in", k16, ar(base + h * 128, base + (h + 1) * 128)))
        for h in range(6):
            P.append(("w_in", k16, perm64(ar(base + h * 128, base + (h + 1) * 128))))
    for h in range(6):
        P.append(("w_in", k16, ar(2368 + h * 128, 2368 + (h + 1) * 128)))
    for i in range(8):
        P.append(("w_in", k16, ar(3136 + i * 128, 3136 + (i + 1) * 128)))
    for h in range(6):
        P.append(("w_uq", [0, 1, 2, 3], ar(h * 192, h * 192 + 128)))
        P.append(("w_uq", [0, 1, 2, 3], ar(h * 192 + 128, h * 192 + 192)))
        P.append(("w_uq", [0, 1, 2, 3], perm64(ar(h * 192 + 128, h * 192 + 192))))
    for h in range(6):
        P.append(("w_ukv", [0, 1], ar(h * 256, h * 256 + 128)))
        P.append(("w_ukv", [0, 1], ar(h * 256 + 128, h * 256 + 256)))
    for m in range(KD):
        P.append(("w_out", k16, ar(m * 128, (m + 1) * 128)))
    ffn("w_ffn2_in", "w_ffn2_out")
    assert len(P) <= SLAB_PER_LAYER, len(P)
    return P


PLAN = slab_plan()
SB_FFN1 = 0
SB_INP = 152
SB_UQ = SB_INP + 46
SB_UKV = SB_UQ + 18
SB_OUT = SB_UKV + 12
SB_FFN2 = SB_OUT + 16


def build_program():
    nc = bass.Bass("TRN2", target_bir_lowering=False)
    es = contextlib.ExitStack()

    def din(name, shape, dt=F32):
        if CFG.get("synth") and name in ("xT", "wsl", "wada"):
            return nc.dram_tensor(name, list(shape), dt, kind="Internal").ap()
        return nc.dram_tensor(name, list(shape), dt, kind="ExternalInput").ap()

    def dint(name, shape, dt):
        return nc.dram_tensor(name, list(shape), dt, kind="Internal").ap()

    xT_d = din("xT", [D, NTOK])
    cT_d = din("cT", [128, KD, NSEQ])
    wsl_d = din("wsl", [2 * SLAB_PER_RANK * 128, 2048])
    wada_d = din("wada", [2 * OC_PER_RANK * 128, KD * 128])
    bada_d = din("bada", [128, 2, 144, NSEQ])
    gn6_d = din("gn6", [128, 2, 4, KD, NSEQ])
    gn3_d = din("gn3", [128, 2, KD])
    vecs_d = din("vecs", [128, 2, 32])
    lam_d = din("lam", [64, 2, 4])
    wdw_d = din("wdw", [128, 2, 4, 31])
    cvec_d = din("cvec", [128, 2, 4, 3])
    cs_d = din("cs", [NT, 128, 2, TT])
    msk_d = din("msk", [128, 16])
    id_d = din("ident", [128, 128])
    yT_d = nc.dram_tensor("yT", [D, NTOK], F32, kind=("Internal" if CFG.get("synth") else "ExternalOutput")).ap()
    if CFG.get("synth"):
        dbg_d = nc.dram_tensor("dbg", [128, 16], F32, kind="ExternalOutput").ap()

    wloc_d = [dint("wloc%d" % l, [SLAB_PER_RANK * 128, 2048], BF16) for l in range(2)]
    wall_d = [dint("wall%d" % l, [SLAB_PER_LAYER * 128, 2048], BF16) for l in range(2)]
    modloc_d = dint("modloc", [2 * OC_PER_RANK * 128, NSEQ], F32)
    modall_d = dint("modall", [NCORES * 2 * OC_PER_RANK * 128, NSEQ], F32)
    x1_d = dint("x1s", [D, NTOK], F32)
    xm_d = dint("xmid", [D, NTOK], F32)
    q_d = dint("qs", [NT * NQIMG * 128, TT], BF16)
    kvloc_a = dint("kvloc_a", [NT * 18 * 128, TT], BF16)
    kvall_a = dint("kvall_a", [NCORES * NT * 18 * 128, TT], BF16)
    kvloc_b = dint("kvloc_b", [NT * 12 * 128, TT], BF16)
    kvall_b = dint("kvall_b", [NCORES * NT * 12 * 128, TT], BF16)
    glup_d = dint("glupad", [512, NT * (TT + 30)], F32)
    eloc_d = dint("eloc", [512, NT * 30], F32)
    eall_d = dint("eall", [NCORES * 512, NT * 30], F32)

    with es:
        def sb(name, shape, dt):
            return es.enter_context(nc.sbuf_tensor(name, list(shape), dt))

        def psm(name, shape, dt):
            return es.enter_context(nc.psum_tensor(name, list(shape), dt))

        NWB = 4
        wbuf = sb("wbuf", [128, NWB, 2048], BF16)
        xT = sb("xT_sb", [128, KD, TT], F32)
        hT = sb("hT_sb", [128, KD, TT], BF16)
        uT = sb("uT_sb", [128, NJ, TT], BF16)
        NF = 6
        fpool = sb("fpool", [128, NF, TT], F32)
        NR = 3
        rpool = sb("rpool", [128, NR, TT], F32)
        NB = 8
        bpool = sb("bpool", [128, NB, TT], BF16)
        cqn = sb("cqn", [128, 4, TT], BF16)
        ckvn = sb("ckvn", [128, 2, TT], BF16)
        krs = sb("krs", [128, 1, TT], F32)
        krsq = sb("krsq", [128, TT], BF16)
        cs = sb("cs_sb", [128, 2, TT], F32)
        qbuf = sb("qbuf", [128, 2, 3, TT], BF16)
        kvb = sb("kvb", [128, 2, 3, 2, TT], BF16)
        glu_sb = sb("glu_sb", [128, 4, TT + 30], F32)
        cacc = sb("cacc", [128, 4, TT], F32)
        modS = sb("modS", [128, 2, 144, NSEQ], F32)
        der = sb("der", [128, 2, 9, KD, NSEQ], F32)
        gn6 = sb("gn6_sb", [128, 2, 4, KD, NSEQ], F32)
        gn3 = sb("gn3_sb", [128, 2, KD], F32)
        vecs = sb("vecs_sb", [128, 2, 32], F32)
        lamt = sb("lam_sb", [128, 2, 8], F32)
        wdw = sb("wdw_sb", [128, 2, 4, 31], F32)
        cvec = sb("cvec_sb", [128, 2, 4, 3], F32)
        msk = sb("msk_sb", [128, 16], F32)
        identf = sb("identf", [128, 128], F32)
        ident = sb("ident_sb", [128, 128], BF16)
        ones_b = sb("ones_b", [128, 128], BF16)
        ones_f = sb("ones_f", [128, 128], F32)
        bd_b = sb("bd_b", [128, 128], BF16)
        epsc = sb("epsc", [128, 1], F32)
        cTs = sb("cT_sb", [128, KD, NSEQ], F32)
        wada_sb = uT[:, 0:16, :].bitcast(F32).rearrange("p a (b c) -> p (a b) c", b=1).rearrange("p (x a) c -> p x (a c)", x=2)
        ebuf = glu_sb[:, :, :].bitcast(F32).rearrange("p a c -> p (a c)")[:, 0:NCORES * 180].rearrange("p (r n) -> p r n", r=NCORES)
        halo = sb("halo", [128, 4, 180], F32)

        pg = psm("pg", [128, 4, TT], F32)
        pacc = psm("pacc", [128, 2, TT], F32)
        pst = psm("pst", [128, TT], F32)
        ptr = psm("ptr", [128, 2 * TT], BF16)

        blk = es.enter_context(nc.Block())
        S = Sched(nc, es)

        def TL(prefix, n):
            return [T("%s%d" % (prefix, i)) for i in range(n)]
        Twb = TL("wb", NWB); wslot = [S.slot("wsl%d" % i) for i in range(NWB)]
        TxT = TL("xT", KD); ThT = TL("hT", KD); TuT = TL("uT", NJ)
        Tf = TL("f", NF); Tr = TL("r", NR); Tb = TL("b", NB); bslot = [S.slot("bsl%d" % i) for i in range(NB)]
        Tcqn = TL("cqn", 4); Tckvn = TL("ckvn", 2); Tkrs = T("krs"); Tkrsq = T("krsq"); Tcs = T("cs")
        Tq = TL("q", 2); qslot = [S.slot("qsl%d" % i) for i in range(2)]
        Tkv = TL("kv", 2); kvslot = [S.slot("kvsl%d" % i) for i in range(2)]
        Tglu = T("glu"); Tcacc = TL("cacc", 4)
        Tpg = TL("pg", 4); Tpacc = TL("pacc", 2); Tpst = T("pst"); Tptr = T("ptr")
        for t_ in Tpg + Tpacc + [Tpst, Tptr]:
            t_.excl = True
        Tconst = T("const"); Tmod = T("mod"); Tder = T("der"); Tlam = T("lam")
        Twloc = TL("wloc", 2); Twall = TL("wall", 2)
        Tx1d = TL("x1d", NT); Txmd = TL("xmd", NT); Tyd = TL("yd", NT)
        Tqd = TL("qd", NT); Tkvloc = T("kvloc"); Tkvall = T("kvall"); TkvlocB = T("kvlocb"); TkvallB = T("kvallb")
        Tglup = T("glup"); Teloc = T("eloc"); Teall = T("eall"); Tebuf = T("ebuf"); Thalo = T("halo")
        Tmodloc = T("modloc"); Tmodall = T("modall"); Twada = TL("wada", 2); Tcts = T("cts")
        sl_x = S.slot("xio"); sl_y = S.slot("yio"); sl_c = [S.slot("c%d" % i) for i in range(4)]
        sl_wada = [S.slot("wada%d" % i) for i in range(2)]
        sl_cast = S.slot("cast"); sl_cc = S.slot("cc", 1); sl_glu = S.slot("glu"); sl_e = S.slot("edge")
        sl_misc = S.slot("misc")

        rot = {"f": 0, "b": 0, "pg": 0, "w": 0, "r": 0}

        def nr():
            i = rot["r"]; rot["r"] = (i + 1) % NR
            return rpool[:, i, :], Tr[i]

        def nf():
            i = rot["f"]; rot["f"] = (i + 1) % NF
            return fpool[:, i, :], Tf[i]

        def nb():
            i = rot["b"]; rot["b"] = (i + 1) % NB
            return bpool[:, i, :], Tb[i], bslot[i]

        def npg():
            i = rot["pg"]; rot["pg"] = (i + 1) % 4
            return pg[:, i, :], Tpg[i]

        class WS:
            order = []
            issued = 0
            pos = 0
            live = {}

        def w_issue(idx):
            l, sidx = WS.order[idx]
            b = idx % NWB
            src = wall_d[l][sidx * 128:(sidx + 1) * 128, :]
            S.dma("sp", wslot[b], wbuf[:, b, :], src, reads=[Twall[l]], writes=[Twb[b]])

        def wget(l, sidx):
            if S.dry:
                WS.order.append((l, sidx))
                return wbuf[:, 0, :], Twb[0]
            idx = WS.pos
            assert WS.order[idx] == (l, sidx), (idx, WS.order[idx], l, sidx)
            while WS.issued <= min(idx + NWB - 2, len(WS.order) - 1):
                w_issue(WS.issued); WS.issued += 1
            if WS.issued <= idx:
                w_issue(idx); WS.issued = idx + 1
            WS.pos += 1
            b = idx % NWB
            return wbuf[:, b, :], Twb[b]

        def mm(out_ap, pairs, reads, wt):
            def fn(e, out_ap=out_ap, pairs=pairs):
                n = len(pairs)
                ins = None
                for i, (l, r) in enumerate(pairs):
                    ins = e.matmul(out_ap, lhsT=l, rhs=r, start=(i == 0), stop=(i == n - 1))
                return ins
            S.op("pe", fn, reads=reads, writes=[wt])

        def proj(l, sidx, kc, m, rhs_fn, rhs_tiles, out_ap, out_t, rows=128):
            w, tw = wget(l, sidx)
            pairs = [(w[0:rows, k * m:(k + 1) * m], rhs_fn(k)) for k in range(kc)]
            mm(out_ap, pairs, [tw] + list(rhs_tiles), out_t)

        def rstd_from(ps_ap, rows, scale, out_ap, out_t, src_t):
            S.op("act", lambda e: e.activation(out=out_ap, in_=ps_ap, func=AF.Ln, scale=scale, bias=epsc[0:rows, 0:1]),
                 reads=[src_t, Tconst], writes=[out_t])
            S.op("act", lambda e: e.activation(out=out_ap, in_=out_ap, func=AF.Exp, scale=-0.5),
                 reads=[out_t], writes=[out_t])

        def body():
            S.dma("sp", sl_c[0], identf[:], id_d[:, :], writes=[Tconst])
            S.dma("sp", sl_c[1], cTs[:], cT_d[:, :, :], writes=[Tcts])
            S.dma("sp", sl_c[2], gn6[:], gn6_d[:, :, :, :, :], writes=[Tconst])
            S.dma("sp", sl_c[3], gn3[:], gn3_d[:, :, :], writes=[Tconst])
            S.dma("sp", sl_c[0], vecs[:], vecs_d[:, :, :], writes=[Tconst])
            S.dma("sp", sl_c[1], lamt[0:64, :, 0:4], lam_d[:, :, :], writes=[Tlam])
            S.dma("sp", sl_c[2], wdw[:], wdw_d[:, :, :, :], writes=[Tconst])
            S.dma("sp", sl_c[3], cvec[:], cvec_d[:, :, :, :], writes=[Tconst])
            S.dma("sp", sl_c[0], msk[:], msk_d[:, :], writes=[Tconst])
            S.op("dve", lambda e: e.tensor_copy(out=ident[:], in_=identf[:]), reads=[Tconst], writes=[Tconst])
            S.op("dve", lambda e: e.memset(ones_b[:], 1.0), writes=[Tconst])
            S.op("dve", lambda e: e.memset(ones_f[:], 1.0), writes=[Tconst])
            S.op("dve", lambda e: e.memset(bd_b[:], 0.0), writes=[Tconst])
            S.op("dve", lambda e: e.memset(bd_b[0:64, 0:64], 1.0), reads=[Tconst], writes=[Tconst])
            S.op("dve", lambda e: e.memset(bd_b[64:128, 64:128], 1.0), reads=[Tconst], writes=[Tconst])
            S.op("dve", lambda e: e.memset(epsc[:], EPS), writes=[Tconst])

            for l in range(2):
                for ch in range(5):
                    r0 = (l * SLAB_PER_RANK + ch * 10) * 128
                    o0 = ch * 10 * 128
                    S.dma("pool", sl_cast, wloc_d[l][o0:o0 + 1280, :], wsl_d[r0:r0 + 1280, :], writes=[Twloc[l]])
                S.custom("pool", sl_cc, lambda e, l=l: e.collective_compute(
                    "AllGather", ALU.bypass, replica_groups=[list(range(NCORES))],
                    ins=[wloc_d[l][:, :]], outs=[wall_d[l][:, :]]), reads=[Twloc[l]], writes=[Twall[l]])

            S.op("act", lambda e: e.activation(out=cTs[:], in_=cTs[:], func=AF.Silu), reads=[Tcts], writes=[Tcts])
            for l in range(2):
                for j in range(OC_PER_RANK):
                    i = l * OC_PER_RANK + j
                    b = i % 2
                    S.dma("sp", sl_wada[b], wada_sb[:, b, :], wada_d[i * 128:(i + 1) * 128, :], writes=[Twada[b]])
                    mm(pst[:, j * NSEQ:(j + 1) * NSEQ],
                       [(wada_sb[:, b, k * 128:(k + 1) * 128], cTs[:, k, :]) for k in range(KD)],
                       [Twada[b], Tcts], Tpst)
                fa, tfa = nf()
                S.op("dve", lambda e, fa=fa: e.tensor_copy(out=fa[:, 0:OC_PER_RANK * NSEQ], in_=pst[:, 0:OC_PER_RANK * NSEQ]),
                     reads=[Tpst], writes=[tfa])
                dst = modloc_d[l * OC_PER_RANK * 128:(l + 1) * OC_PER_RANK * 128, :].rearrange("(j p) s -> p j s", p=128)
                S.dma("sp", sl_misc, dst, fa[:, 0:OC_PER_RANK * NSEQ].rearrange("p (j s) -> p j s", s=NSEQ),
                      reads=[tfa], writes=[Tmodloc])
            S.custom("pool", sl_cc, lambda e: e.collective_compute(
                "AllGather", ALU.bypass, replica_groups=[list(range(NCORES))],
                ins=[modloc_d[:, :]], outs=[modall_d[:, :]]), reads=[Tmodloc], writes=[Tmodall])
            mview = modall_d.rearrange("(r l j p) s -> p l r j s", r=NCORES, l=2, j=OC_PER_RANK, p=128)
            for l in range(2):
                for r in range(NCORES):
                    S.dma("sp", sl_misc, modS[:, l, r * OC_PER_RANK:(r + 1) * OC_PER_RANK, :], mview[:, l, r, :, :],
                          reads=[Tmodall], writes=[Tmod])
            fb, tfb = nf()
            for l in range(2):
                S.dma("sp", sl_misc, wada_sb[:, 0, 0:144 * NSEQ], bada_d[:, l, :, :].rearrange("p a s -> p (a s)"),
                      reads=[], writes=[Twada[0]])
                S.op("dve", lambda e, l=l: e.tensor_tensor(out=modS[:, l, :, :].rearrange("p a s -> p (a s)"),
                                                           in0=modS[:, l, :, :].rearrange("p a s -> p (a s)"),
                                                           in1=wada_sb[:, 0, 0:144 * NSEQ], op=ALU.add),
                     reads=[Tmod, Twada[0]], writes=[Tmod])
                for i in range(3):
                    S.op("dve", lambda e, l=l, i=i: e.scalar_tensor_tensor(
                        out=der[:, l, 3 * i, :, :].rearrange("p k s -> p (k s)"),
                        in0=modS[:, l, (3 * i + 1) * KD:(3 * i + 2) * KD, :].rearrange("p k s -> p (k s)"), scalar=1.0,
                        in1=gn6[:, l, i, :, :].rearrange("p k s -> p (k s)"), op0=ALU.add, op1=ALU.mult),
                        reads=[Tmod, Tconst], writes=[Tder])
                    S.op("dve", lambda e, l=l, i=i: e.tensor_copy(
                        out=der[:, l, 3 * i + 1, :, :].rearrange("p k s -> p (k s)"),
                        in_=modS[:, l, (3 * i) * KD:(3 * i + 1) * KD, :].rearrange("p k s -> p (k s)")),
                        reads=[Tmod], writes=[Tder])
                    S.op("dve", lambda e, l=l, i=i: e.tensor_scalar(
                        out=der[:, l, 3 * i + 2, :, :].rearrange("p k s -> p (k s)"),
                        in0=modS[:, l, (3 * i + 2) * KD:(3 * i + 3) * KD, :].rearrange("p k s -> p (k s)"),
                        scalar1=(1.0 if i == 1 else 0.5), scalar2=None, op0=ALU.mult),
                        reads=[Tmod], writes=[Tder])
                S.op("dve", lambda e, l=l: e.tensor_tensor(out=lamt[0:64, l, 4:5], in0=lamt[0:64, l, 0:1], in1=lamt[0:64, l, 1:2], op=ALU.mult),
                     reads=[Tlam], writes=[Tlam])
                S.op("dve", lambda e, l=l: e.tensor_tensor(out=lamt[0:64, l, 5:6], in0=lamt[0:64, l, 2:3], in1=lamt[0:64, l, 3:4], op=ALU.mult),
                     reads=[Tlam], writes=[Tlam])
                mm(pst[:, 0:2], [(ones_f[0:64, :], lamt[0:64, l, 4:6])], [Tconst, Tlam], Tpst)
                S.op("act", lambda e, l=l: e.activation(out=lamt[:, l, 6:8], in_=pst[:, 0:2], func=AF.Exp), reads=[Tpst], writes=[Tlam])
                lam_init = 0.8 - 0.6 * math.exp(-0.3 * l)
                S.op("dve", lambda e, l=l, li=lam_init: e.scalar_tensor_tensor(
                    out=lamt[:, l, 4:5], in0=lamt[:, l, 7:8], scalar=-li, in1=lamt[:, l, 6:7], op0=ALU.add, op1=ALU.subtract),
                    reads=[Tlam], writes=[Tlam])
                S.op("dve", lambda e, l=l, li=lam_init: e.tensor_scalar(out=vecs[:, l, 12:13], in0=vecs[:, l, 12:13],
                                                                         scalar1=(1.0 - li), scalar2=None, op0=ALU.mult),
                     reads=[Tconst], writes=[Tconst])

            def D_(l, q, k, s):
                return der[:, l, q, k, s:s + 1]

            def norm_to_h(l, qi, s, g3=None):
                for k in range(KD):
                    bq, tb, _ = nb()
                    S.op("act", lambda e, k=k, bq=bq: e.activation(out=bq, in_=xT[:, k, :], func=AF.Square),
                         reads=[TxT[k]], writes=[tb])
                    S.op("pe", lambda e, k=k, bq=bq: e.matmul(pst[:, :], lhsT=ones_b[:, :], rhs=bq, start=(k == 0), stop=(k == KD - 1)),
                         reads=[tb, Tconst], writes=[Tpst] if k == 0 else [Tpst])
                rs, trs = nr()
                rstd_from(pst[:, :], 128, 1.0 / D, rs, trs, Tpst)
                for k in range(KD):
                    if g3 is None:
                        tmp, tt = nf()
                        S.op("dve", lambda e, k=k, tmp=tmp, rs=rs: e.tensor_tensor(out=tmp, in0=xT[:, k, :], in1=rs, op=ALU.mult),
                             reads=[TxT[k], trs], writes=[tt])
                        S.op("act", lambda e, k=k, tmp=tmp: e.activation(out=hT[:, k, :], in_=tmp, func=AF.Identity,
                                                                          scale=D_(l, qi, k, s), bias=D_(l, qi + 1, k, s)),
                             reads=[tt, Tder], writes=[ThT[k]])
                    else:
                        S.op("dve", lambda e, k=k, rs=rs: e.scalar_tensor_tensor(out=xT[:, k, :], in0=xT[:, k, :], scalar=gn3[:, l, k:k + 1],
                                                                                 in1=rs, op0=ALU.mult, op1=ALU.mult),
                             reads=[TxT[k], trs, Tconst], writes=[TxT[k]])

            def ffn(l, sbase, qi, s):
                norm_to_h(l, qi, s)
                for j in range(NJ):
                    pa, tpa = npg(); pb, tpb = npg()
                    proj(l, sbase + 2 * j, KD, 128, lambda k: hT[:, k, :], ThT, pa, tpa)
                    proj(l, sbase + 2 * j + 1, KD, 128, lambda k: hT[:, k, :], ThT, pb, tpb)
                    sa, tsa = nf()
                    S.op("act", lambda e, pa=pa, sa=sa: e.activation(out=sa, in_=pa, func=AF.Silu), reads=[tpa], writes=[tsa])
                    S.op("dve", lambda e, j=j, sa=sa, pb=pb: e.tensor_tensor(out=uT[:, j, :], in0=sa, in1=pb, op=ALU.mult),
                         reads=[tsa, tpb], writes=[TuT[j]])
                for m in range(KD):
                    po, tpo = npg()
                    for q in range(4):
                        w, tw = wget(l, sbase + 88 + m * 4 + q)
                        def fn(e, w=w, q=q, po=po):
                            ins = None
                            for kk in range(11):
                                ins = e.matmul(po, lhsT=w[:, kk * 128:(kk + 1) * 128], rhs=uT[:, q * 11 + kk, :],
                                               start=(q == 0 and kk == 0), stop=(q == 3 and kk == 10))
                            return ins
                        S.op("pe", fn, reads=[tw] + TuT[q * 11:(q + 1) * 11], writes=[tpo])
                    S.op("dve", lambda e, m=m, po=po: e.scalar_tensor_tensor(out=xT[:, m, :], in0=po, scalar=D_(l, qi + 2, m, s),
                                                                              in1=xT[:, m, :], op0=ALU.mult, op1=ALU.add),
                         reads=[tpo, TxT[m], Tder], writes=[TxT[m]])

            def load_x(src_d, t, tsrc):
                S.dma("sp", sl_x, xT[:, :, :], src_d.rearrange("(k p) n -> p k n", p=128)[:, :, t * TT:(t + 1) * TT],
                      reads=[tsrc], writes=TxT)

            def store_x(dst_d, t, tdst):
                S.dma("sp", sl_y, dst_d.rearrange("(k p) n -> p k n", p=128)[:, :, t * TT:(t + 1) * TT], xT[:, :, :],
                      reads=TxT, writes=[tdst])

            def img_store(dram, row0, rows, src_ap, src_t, slot, dt):
                S.dma("pool", slot, dram[row0:row0 + rows, :], src_ap, reads=[src_t], writes=[dt])

            def rope_combine(l, praw, tpraw, pperm, tpperm, rows, gcol, gpcol, rs, trs, out_ap, out_t):
                t1, tt1 = nf(); t2, tt2 = nf()
                S.op("dve", lambda e: e.scalar_tensor_tensor(out=t1[0:rows, :], in0=praw, scalar=vecs[0:rows, l, gcol:gcol + 1],
                                                             in1=cs[0:rows, 0, :], op0=ALU.mult, op1=ALU.mult),
                     reads=[tpraw, Tcs, Tconst], writes=[tt1])
                S.op("dve", lambda e: e.scalar_tensor_tensor(out=t2[0:rows, :], in0=pperm, scalar=vecs[0:rows, l, gpcol:gpcol + 1],
                                                             in1=cs[0:rows, 1, :], op0=ALU.mult, op1=ALU.mult),
                     reads=[tpperm, Tcs, Tconst], writes=[tt2])
                S.op("dve", lambda e: e.tensor_tensor(out=t1[0:rows, :], in0=t1[0:rows, :], in1=t2[0:rows, :], op=ALU.add),
                     reads=[tt1, tt2], writes=[tt1])
                if rs is None:
                    return t1, tt1
                S.op("dve", lambda e: e.tensor_tensor(out=out_ap, in0=t1[0:rows, :], in1=rs, op=ALU.mult),
                     reads=[tt1, trs], writes=[out_t])

            def sumsq_group(parts, ones_ap, rows_out=128):
                n = len(parts)
                for i, (pap, tp, rows) in enumerate(parts):
                    bq, tb, _ = nb()
                    S.op("act", lambda e, pap=pap, bq=bq, rows=rows: e.activation(out=bq[0:rows, :], in_=pap, func=AF.Square),
                         reads=[tp], writes=[tb])
                    S.op("pe", lambda e, bq=bq, rows=rows, i=i: e.matmul(pst[0:rows_out, :], lhsT=ones_ap[0:rows, 0:rows_out], rhs=bq[0:rows, :],
                                                                         start=(i == 0), stop=(i == n - 1)),
                         reads=[tb, Tconst], writes=[Tpst])

            def phaseA(l, t, src_d, tsrc):
                s = SEQ_OF_TILE[t]
                load_x(src_d, t, tsrc)
                ffn(l, SB_FFN1, 0, s)
                store_x(x1_d, t, Tx1d[t])
                if CFG.get('sub', 99) <= 1:
                    return
                norm_to_h(l, 3, s)
                S.dma("sp", sl_misc, cs[:, :, :], cs_d[t, :, :, :], writes=[Tcs])
                hrhs = lambda k: hT[:, k, :]
                qrow = t * NQIMG * 128
                kvrow = t * 18 * 128
                kvrowb = t * 12 * 128
                pcs = []
                for i in range(4):
                    p, tp = npg()
                    proj(l, SB_INP + i, KD, 128, hrhs, ThT, p, tp)
                    pcs.append((p, tp, 128))
                sumsq_group(pcs, ones_b)
                rs, trs = nr()
                rstd_from(pst[:, :], 128, 1.0 / 512, rs, trs, Tpst)
                for i in range(4):
                    S.op("dve", lambda e, i=i, p=pcs[i][0], rs=rs: e.scalar_tensor_tensor(out=cqn[:, i, :], in0=p, scalar=vecs[:, l, 16 + i:17 + i],
                                                                                           in1=rs, op0=ALU.mult, op1=ALU.mult),
                         reads=[pcs[i][1], trs, Tconst], writes=[Tcqn[i]])
                pcs = []
                for i in range(2):
                    p, tp = npg()
                    proj(l, SB_INP + 4 + i, KD, 128, hrhs, ThT, p, tp)
                    pcs.append((p, tp, 128))
                sumsq_group(pcs, ones_b)
                rs, trs = nr()
                rstd_from(pst[:, :], 128, 1.0 / 256, rs, trs, Tpst)
                for i in range(2):
                    S.op("dve", lambda e, i=i, p=pcs[i][0], rs=rs: e.scalar_tensor_tensor(out=ckvn[:, i, :], in0=p, scalar=vecs[:, l, 20 + i:21 + i],
                                                                                           in1=rs, op0=ALU.mult, op1=ALU.mult),
                         reads=[pcs[i][1], trs, Tconst], writes=[Tckvn[i]])
                if CFG.get('sub', 99) <= 2:
                    return
                p1, tp1 = npg(); p2, tp2 = npg()
                proj(l, SB_INP + 6, KD, 128, hrhs, ThT, p1, tp1)
                proj(l, SB_INP + 7, KD, 128, hrhs, ThT, p2, tp2)
                if CFG.get('sub', 99) == 2.3:
                    return
                S.op("act", lambda e, p1=p1: e.activation(out=krsq[0:64, :], in_=p1[0:64, :], func=AF.Square), reads=[tp1], writes=[Tkrsq])
                if CFG.get('sub', 99) == 2.6:
                    return
                kr, tkr = rope_combine(l, p1[0:64, :], tp1, p2[0:64, :], tp2, 64, 4, 5, None, None, None, None)
                S.op("dve", lambda e, kr=kr: e.tensor_copy(out=krs[0:64, 0, :], in_=kr[0:64, :]), reads=[tkr], writes=[Tkrs])
                if CFG.get('sub', 99) <= 3:
                    return
                for which, base, gcol, dst_is_q in (("dq", SB_INP + 8, 6, True), ("dk", SB_INP + 20, 8, False)):
                    for h in range(6):
                        p1, tp1 = npg(); p2, tp2 = npg()
                        proj(l, base + h, KD, 128, hrhs, ThT, p1, tp1)
                        proj(l, base + 6 + h, KD, 128, hrhs, ThT, p2, tp2)
                        sumsq_group([(p1, tp1, 128)], bd_b)
                        rs, trs = nr()
                        rstd_from(pst[:, :], 128, 1.0 / 64, rs, trs, Tpst)
                        ob, tob, osl = nb()
                        rope_combine(l, p1, tp1, p2, tp2, 128, gcol, gcol + 1, rs, trs, ob, tob)
                        if dst_is_q:
                            img_store(q_d, qrow + (12 + h) * 128, 128, ob, tob, osl, Tqd[t])
                        else:
                            img_store(kvloc_b, kvrowb + (2 * h) * 128, 128, ob, tob, osl, TkvlocB)
                if CFG.get('sub', 99) <= 4:
                    return
                def v_image(pv, tpv, row, kvd=None, tkvd=None):
                    kvd = kvloc_a if kvd is None else kvd
                    tkvd = Tkvloc if tkvd is None else tkvd
                    vb, tvb, _ = nb()
                    S.op("act", lambda e: e.activation(out=vb, in_=pv, func=AF.Identity), reads=[tpv], writes=[tvb])
                    def fn(e):
                        ins = None
                        for c in range(4):
                            ins = e.transpose(ptr[:, c * 128:(c + 1) * 128], vb[:, c * 128:(c + 1) * 128], ident[:, :])
                        return ins
                    S.op("pe", fn, reads=[tvb, Tconst], writes=[Tptr])
                    ob, tob, osl = nb()
                    S.op("dve", lambda e: e.tensor_copy(out=ob, in_=ptr[:, 0:TT]), reads=[Tptr], writes=[tob])
                    img_store(kvd, row, 128, ob, tob, osl, tkvd)
                for h in range(6):
                    pv, tpv = npg()
                    proj(l, SB_INP + 32 + h, KD, 128, hrhs, ThT, pv, tpv)
                    v_image(pv, tpv, kvrowb + (2 * h + 1) * 128, kvloc_b, TkvlocB)
                if CFG.get('sub', 99) <= 5:
                    return
                for c in range(4):
                    pa, tpa = npg(); pgt, tpgt = npg()
                    proj(l, SB_INP + 38 + c, KD, 128, hrhs, ThT, pa, tpa)
                    proj(l, SB_INP + 42 + c, KD, 128, hrhs, ThT, pgt, tpgt)
                    sg, tsg = nf()
                    S.op("act", lambda e, sg=sg, pgt=pgt: e.activation(out=sg, in_=pgt, func=AF.Sigmoid), reads=[tpgt], writes=[tsg])
                    S.op("dve", lambda e, c=c, sg=sg, pa=pa: e.tensor_tensor(out=cacc[:, c, :], in0=pa, in1=sg, op=ALU.mult),
                         reads=[tpa, tsg], writes=[Tcacc[c]])
                gp = glup_d.rearrange("(c p) n -> p c n", p=128)
                c0 = t * (TT + 30)
                S.dma("pool", sl_glu, gp[:, :, c0 + 15:c0 + 15 + TT], cacc[:, :, :], reads=Tcacc, writes=[Tglup])
                ep = eloc_d.rearrange("(c p) n -> p c n", p=128)
                S.dma("pool", sl_e, ep[:, :, t * 30:t * 30 + 15], cacc[:, :, 0:15], reads=Tcacc, writes=[Teloc])
                S.dma("pool", sl_e, ep[:, :, t * 30 + 15:t * 30 + 30], cacc[:, :, TT - 15:TT], reads=Tcacc, writes=[Teloc])
                if t > 0 and SEQ_OF_TILE[t - 1] == s:
                    cp = (t - 1) * (TT + 30)
                    S.dma("pool", sl_e, gp[:, :, cp + 15 + TT:cp + 30 + TT], cacc[:, :, 0:15], reads=Tcacc, writes=[Tglup])
                if t < NT - 1 and SEQ_OF_TILE[t + 1] == s:
                    cn = (t + 1) * (TT + 30)
                    S.dma("pool", sl_e, gp[:, :, cn:cn + 15], cacc[:, :, TT - 15:TT], reads=Tcacc, writes=[Tglup])
                if CFG.get('sub', 99) <= 6:
                    return
                crhs = lambda k: cqn[:, k, :]
                for h in range(6):
                    pn, tpn = npg(); pr, tpr = npg(); pp, tpp = npg()
                    proj(l, SB_UQ + 3 * h, 4, 128, crhs, Tcqn, pn, tpn)
                    proj(l, SB_UQ + 3 * h + 1, 4, 128, crhs, Tcqn, pr, tpr)
                    proj(l, SB_UQ + 3 * h + 2, 4, 128, crhs, Tcqn, pp, tpp)
                    sumsq_group([(pn, tpn, 128), (pr[0:64, :], tpr, 64)], ones_b)
                    rs, trs = nr()
                    rstd_from(pst[:, :], 128, 1.0 / 192, rs, trs, Tpst)
                    ob, tob, osl = nb()
                    S.op("dve", lambda e, pn=pn, rs=rs, ob=ob: e.scalar_tensor_tensor(out=ob, in0=pn, scalar=vecs[:, l, 0:1], in1=rs,
                                                                                      op0=ALU.mult, op1=ALU.mult),
                         reads=[tpn, trs, Tconst], writes=[tob])
                    img_store(q_d, qrow + (2 * h) * 128, 128, ob, tob, osl, Tqd[t])
                    ob2, tob2, osl2 = nb()
                    rope_combine(l, pr[0:64, :], tpr, pp[0:64, :], tpp, 64, 1, 2, rs[0:64, :], trs, ob2[0:64, :], tob2)
                    img_store(q_d, qrow + (2 * h + 1) * 128, 64, ob2[0:64, :], tob2, osl2, Tqd[t])
                if CFG.get('sub', 99) <= 7:
                    return
                krhs = lambda k: ckvn[:, k, :]
                for h in range(6):
                    pn, tpn = npg()
                    proj(l, SB_UKV + 2 * h, 2, 128, krhs, Tckvn, pn, tpn)
                    bq, tb, _ = nb()
                    S.op("act", lambda e, pn=pn, bq=bq: e.activation(out=bq, in_=pn, func=AF.Square), reads=[tpn], writes=[tb])
                    S.op("pe", lambda e, bq=bq: e.matmul(pst[:, :], lhsT=ones_b[:, :], rhs=bq, start=True, stop=False),
                         reads=[tb, Tconst], writes=[Tpst])
                    S.op("pe", lambda e: e.matmul(pst[:, :], lhsT=ones_b[0:64, :], rhs=krsq[0:64, :], start=False, stop=True),
                         reads=[Tkrsq, Tconst], writes=[Tpst])
                    rs, trs = nr()
                    rstd_from(pst[:, :], 128, 1.0 / 192, rs, trs, Tpst)
                    ob, tob, osl = nb()
                    S.op("dve", lambda e, pn=pn, rs=rs, ob=ob: e.scalar_tensor_tensor(out=ob, in0=pn, scalar=vecs[:, l, 3:4], in1=rs,
                                                                                      op0=ALU.mult, op1=ALU.mult),
                         reads=[tpn, trs, Tconst], writes=[tob])
                    img_store(kvloc_a, kvrow + (3 * h) * 128, 128, ob, tob, osl, Tkvloc)
                    ob2, tob2, osl2 = nb()
                    S.op("dve", lambda e, rs=rs, ob2=ob2: e.tensor_tensor(out=ob2[0:64, :], in0=krs[0:64, 0, :], in1=rs[0:64, :], op=ALU.mult),
                         reads=[Tkrs, trs], writes=[tob2])
                    img_store(kvloc_a, kvrow + (3 * h + 1) * 128, 64, ob2[0:64, :], tob2, osl2, Tkvloc)
                    pv, tpv = npg()
                    proj(l, SB_UKV + 2 * h + 1, 2, 128, krhs, Tckvn, pv, tpv)
                    v_image(pv, tpv, kvrow + (3 * h + 2) * 128)

            kvva = kvall_a.rearrange("(r t i p) n -> p r t i n", r=NCORES, t=NT, i=18, p=128)
            kvvb = kvall_b.rearrange("(r t i p) n -> p r t i n", r=NCORES, t=NT, i=12, p=128)
            qv = q_d.rearrange("(t i p) n -> p t i n", t=NT, i=NQIMG, p=128)

            def attention_head(l, t, h, is_mla):
                s = SEQ_OF_TILE[t]
                ktiles = [tt for tt in range(NT) if SEQ_OF_TILE[tt] == s]
                if len(ktiles) == 4:
                    groups = [(r, ktiles[0] + 2 * g, 2) for r in range(NCORES) for g in range(2)]
                else:
                    groups = [(r, ktiles[0], 1) for r in range(NCORES)]
                qi = h % 2 if is_mla else h % 2
                qb = rot.setdefault("q", 0); rot["q"] = (qb + 1) % 2
                if is_mla:
                    S.dma("sp", qslot[qb], qbuf[:, qb, 0:2, :], qv[:, t, 2 * h:2 * h + 2, :], reads=[Tqd[t]], writes=[Tq[qb]])
                    imgs = [3 * h, 3 * h + 1, 3 * h + 2]
                else:
                    S.dma("sp", qslot[qb], qbuf[:, qb, 0:1, :], qv[:, t, 12 + h:13 + h, :], reads=[Tqd[t]], writes=[Tq[qb]])
                    imgs = [2 * h, 2 * h + 1]
                nmaps = 1 if is_mla else 2
                scale = 1.0 / math.sqrt(192.0) if is_mla else 1.0 / 8.0

                def kv_load(gi):
                    r, t0, ntl = groups[gi]
                    b = rot.setdefault("kv", 0); rot["kv"] = (b + 1) % 2
                    for ii, img in enumerate(imgs):
                        S.dma("pool", kvslot[b], kvb[:, b, ii, 0:ntl, :], (kvva if is_mla else kvvb)[:, r, t0:t0 + ntl, img, :],
                              reads=[Tkvall if is_mla else TkvallB], writes=[Tkv[b]])
                    return b
                dacc = []
                for m_ in range(nmaps):
                    a, ta = nf()
                    dacc.append((a, ta))
                ng = len(groups)
                bnext = kv_load(0)
                first = True
                nchunks_total = sum(g[2] for g in groups) * 4
                cidx = 0
                for gi in range(ng):
                    b = bnext
                    if gi + 1 < ng:
                        bnext = kv_load(gi + 1)
                    ntl = groups[gi][2]
                    for ti in range(ntl):
                        for c in range(4):
                            last = (cidx == nchunks_total - 1)
                            for m_ in range(nmaps):
                                ps, tps = npg()
                                if is_mla:
                                    mm(ps, [(kvb[:, b, 0, ti, c * 128:(c + 1) * 128], qbuf[:, qb, 0, :]),
                                            (kvb[0:64, b, 1, ti, c * 128:(c + 1) * 128], qbuf[0:64, qb, 1, :])],
                                       [Tkv[b], Tq[qb]], tps)
                                    vimg = 2
                                else:
                                    lo = 64 * m_
                                    mm(ps, [(kvb[lo:lo + 64, b, 0, ti, c * 128:(c + 1) * 128], qbuf[lo:lo + 64, qb, 0, :])],
                                       [Tkv[b], Tq[qb]], tps)
                                    vimg = 1
                                pb_, tpb_, _ = nb()
                                S.op("act", lambda e, ps=ps, pb_=pb_: e.activation(out=pb_, in_=ps, func=AF.Exp, scale=scale),
                                     reads=[tps], writes=[tpb_])
                                a, ta = dacc[m_]
                                if first:
                                    S.op("dve", lambda e, a=a, pb_=pb_: e.tensor_copy(out=a, in_=pb_), reads=[tpb_], writes=[ta])
                                else:
                                    S.op("dve", lambda e, a=a, pb_=pb_: e.tensor_tensor(out=a, in0=a, in1=pb_, op=ALU.add),
                                         reads=[tpb_, ta], writes=[ta])
                                S.op("pe", lambda e, m_=m_, b=b, ti=ti, c=c, pb_=pb_, vimg=vimg, first=first, last=last:
                                     e.matmul(pacc[:, m_, :], lhsT=kvb[:, b, vimg, ti, c * 128:(c + 1) * 128], rhs=pb_, start=first, stop=last),
                                     reads=[Tkv[b], tpb_], writes=[Tpacc[m_]])
                            first = False
                            cidx += 1
                outs = []
                for m_ in range(nmaps):
                    a, ta = dacc[m_]
                    mm(pst[:, :], [(ones_f[:, :], a)], [Tconst, ta], Tpst)
                    rc, trc = nf()
                    S.op("dve", lambda e, rc=rc: e.reciprocal(out=rc, in_=pst[:, :]), reads=[Tpst], writes=[trc])
                    if is_mla:
                        S.op("dve", lambda e, rc=rc: e.tensor_tensor(out=hT[:, h, :], in0=pacc[:, 0, :], in1=rc, op=ALU.mult),
                             reads=[Tpacc[0], trc], writes=[ThT[h]])
                    else:
                        S.op("dve", lambda e, rc=rc, m_=m_: e.tensor_tensor(out=rc, in0=pacc[:, m_, :], in1=rc, op=ALU.mult),
                             reads=[Tpacc[m_], trc], writes=[trc])
                        outs.append((rc, trc))
                if not is_mla:
                    (o1, to1), (o2, to2) = outs
                    S.op("dve", lambda e: e.scalar_tensor_tensor(out=o1, in0=o2, scalar=lamt[:, l, 4:5], in1=o1, op0=ALU.mult, op1=ALU.add),
                         reads=[to1, to2, Tlam], writes=[to1])
                    bq, tb, _ = nb()
                    S.op("act", lambda e: e.activation(out=bq, in_=o1, func=AF.Square), reads=[to1], writes=[tb])
                    mm(pst[:, :], [(ones_b[:, :], bq)], [Tconst, tb], Tpst)
                    rs, trs = nr()
                    rstd_from(pst[:, :], 128, 1.0 / 128, rs, trs, Tpst)
                    S.op("dve", lambda e: e.scalar_tensor_tensor(out=hT[:, 6 + h, :], in0=o1, scalar=vecs[:, l, 12:13], in1=rs,
                                                                 op0=ALU.mult, op1=ALU.mult),
                         reads=[to1, trs, Tconst], writes=[ThT[6 + h]])

            def conv_module(l, t):
                gp = glup_d.rearrange("(c p) n -> p c n", p=128)
                c0 = t * (TT + 30)
                S.dma("sp", sl_glu, glu_sb[:, :, :], gp[:, :, c0:c0 + TT + 30], reads=[Tglup], writes=[Tglu])
                for c in range(4):
                    S.op("dve", lambda e, c=c: e.tensor_scalar(out=cacc[:, c, :], in0=glu_sb[:, c, 0:TT], scalar1=wdw[:, l, c, 0:1],
                                                               scalar2=cvec[:, l, c, 0:1], op0=ALU.mult, op1=ALU.add),
                         reads=[Tglu, Tconst], writes=[Tcacc[c]])
                for j in range(1, 31):
                    for c in range(4):
                        S.op("dve", lambda e, c=c, j=j: e.scalar_tensor_tensor(out=cacc[:, c, :], in0=glu_sb[:, c, j:j + TT],
                                                                               scalar=wdw[:, l, c, j:j + 1], in1=cacc[:, c, :],
                                                                               op0=ALU.mult, op1=ALU.add),
                             reads=[Tglu, Tconst, Tcacc[c]], writes=[Tcacc[c]])
                def fnm(e):
                    ins = None
                    for c in range(4):
                        ins = e.matmul(pst[:, :], lhsT=ones_f[:, :], rhs=cacc[:, c, :], start=(c == 0), stop=(c == 3))
                    return ins
                S.op("pe", fnm, reads=[Tconst] + Tcacc, writes=[Tpst])
                mean, tmean = nr()
                S.op("act", lambda e: e.activation(out=mean, in_=pst[:, :], func=AF.Identity, scale=1.0 / 512), reads=[Tpst], writes=[tmean])
                for c in range(4):
                    S.op("dve", lambda e, c=c: e.tensor_tensor(out=cacc[:, c, :], in0=cacc[:, c, :], in1=mean, op=ALU.subtract),
                         reads=[Tcacc[c], tmean], writes=[Tcacc[c]])
                for c in range(4):
                    bq, tb, _ = nb()
                    S.op("act", lambda e, c=c, bq=bq: e.activation(out=bq, in_=cacc[:, c, :], func=AF.Square), reads=[Tcacc[c]], writes=[tb])
                    S.op("pe", lambda e, c=c, bq=bq: e.matmul(pst[:, :], lhsT=ones_b[:, :], rhs=bq, start=(c == 0), stop=(c == 3)),
                         reads=[tb, Tconst], writes=[Tpst])
                rs, trs = nr()
                rstd_from(pst[:, :], 128, 1.0 / 512, rs, trs, Tpst)
                for c in range(4):
                    tmp, tt = nf()
                    S.op("dve", lambda e, c=c, tmp=tmp: e.tensor_tensor(out=tmp, in0=cacc[:, c, :], in1=rs, op=ALU.mult),
                         reads=[Tcacc[c], trs], writes=[tt])
                    S.op("act", lambda e, c=c, tmp=tmp: e.activation(out=hT[:, 12 + c, :], in_=tmp, func=AF.Silu,
                                                                      scale=cvec[:, l, c, 1:2], bias=cvec[:, l, c, 2:3]),
                         reads=[tt, Tconst], writes=[ThT[12 + c]])

            def halo_exchange():
                S.custom("pool", sl_cc, lambda e: e.collective_compute(
                    "AllGather", ALU.bypass, replica_groups=[list(range(NCORES))],
                    ins=[eloc_d[:, :]], outs=[eall_d[:, :]]), reads=[Teloc], writes=[Teall])
                ev = eall_d.rearrange("(r c p) n -> p r c n", r=NCORES, c=4, p=128)
                first_t = [0, 4, 8, 9, 10, 11]
                last_t = [3, 7, 8, 9, 10, 11]
                gp = glup_d.rearrange("(c p) n -> p c n", p=128)
                for c in range(4):
                    for s in range(NSEQ):
                        S.dma("sp", sl_misc, ebuf[:, :, s * 15:(s + 1) * 15], ev[:, :, c, last_t[s] * 30 + 15:last_t[s] * 30 + 30],
                              reads=[Teall], writes=[Tebuf])
                        S.dma("sp", sl_misc, ebuf[:, :, 90 + s * 15:90 + (s + 1) * 15], ev[:, :, c, first_t[s] * 30:first_t[s] * 30 + 15],
                              reads=[Teall], writes=[Tebuf])
                    for half in range(2):
                        for r in range(NCORES):
                            mcol = half * 8 + r
                            if r == 0:
                                S.op("dve", lambda e, c=c, half=half, r=r, mcol=mcol: e.tensor_scalar(
                                    out=halo[:, c, half * 90:(half + 1) * 90], in0=ebuf[:, r, half * 90:(half + 1) * 90],
                                    scalar1=msk[:, mcol:mcol + 1], scalar2=None, op0=ALU.mult), reads=[Tebuf, Tconst], writes=[Thalo])
                            else:
                                S.op("dve", lambda e, c=c, half=half, r=r, mcol=mcol: e.scalar_tensor_tensor(
                                    out=halo[:, c, half * 90:(half + 1) * 90], in0=ebuf[:, r, half * 90:(half + 1) * 90],
                                    scalar=msk[:, mcol:mcol + 1], in1=halo[:, c, half * 90:(half + 1) * 90], op0=ALU.mult, op1=ALU.add),
                                    reads=[Tebuf, Tconst, Thalo], writes=[Thalo])
                for s in range(NSEQ):
                    cl = first_t[s] * (TT + 30)
                    S.dma("sp", sl_misc, gp[:, :, cl:cl + 15], halo[:, :, s * 15:(s + 1) * 15], reads=[Thalo], writes=[Tglup])
                    cr = last_t[s] * (TT + 30) + 15 + TT
                    S.dma("sp", sl_misc, gp[:, :, cr:cr + 15], halo[:, :, 90 + s * 15:90 + (s + 1) * 15], reads=[Thalo], writes=[Tglup])

            def phaseB(l, t, dst_d, tdst):
                s = SEQ_OF_TILE[t]
                for h in range(6):
                    attention_head(l, t, h, True)
                for h in range(6):
                    attention_head(l, t, h, False)
                conv_module(l, t)
                load_x(x1_d, t, Tx1d[t])
                for m in range(KD):
                    po, tpo = npg()
                    proj(l, SB_OUT + m, KD, 128, lambda k: hT[:, k, :], ThT, po, tpo)
                    S.op("dve", lambda e, m=m, po=po: e.scalar_tensor_tensor(out=xT[:, m, :], in0=po, scalar=D_(l, 5, m, s),
                                                                              in1=xT[:, m, :], op0=ALU.mult, op1=ALU.add),
                         reads=[tpo, TxT[m], Tder], writes=[TxT[m]])
                ffn(l, SB_FFN2, 6, s)
                norm_to_h(l, 0, s, g3=True)
                store_x(dst_d[0], t, tdst[t])

            stage = CFG.get("stage", 9)
            for l in range(CFG["L"]):
                if stage < 1:
                    break
                src_d, tsrc = (xT_d, [T("xin")] * NT) if l == 0 else (xm_d, Txmd)
                for t in CFG["tiles"]:
                    phaseA(l, t, src_d, tsrc[t])
                if stage < 2:
                    break
                S.custom("pool", sl_cc, lambda e: e.collective_compute(
                    "AllGather", ALU.bypass, replica_groups=[list(range(NCORES))],
                    ins=[kvloc_a[:, :]], outs=[kvall_a[:, :]]), reads=[Tkvloc], writes=[Tkvall])
                S.custom("pool", sl_cc, lambda e: e.collective_compute(
                    "AllGather", ALU.bypass, replica_groups=[list(range(NCORES))],
                    ins=[kvloc_b[:, :]], outs=[kvall_b[:, :]]), reads=[TkvlocB], writes=[TkvallB])
                halo_exchange()
                if stage < 3:
                    break
                for t in CFG["tiles"]:
                    if l < CFG["L"] - 1:
                        phaseB(l, t, [xm_d], Txmd)
                    else:
                        phaseB(l, t, [yT_d], Tyd)

        S.dry = True
        body()
        S.dry = False
        for k in rot:
            rot[k] = 0
        rot.pop("q", None); rot.pop("kv", None)
        body()
        if CFG.get("synth"):
            S.dma("sp", sl_misc, dbg_d[:, :], msk[:, :], reads=[Tconst], writes=[Tyd[0]])
        allT = Tyd + Txmd + Tx1d + Tqd + Tpg + Tf + Tb + Tr + [Tpst, Tcs, Tkrs, Tkrsq, Tkvloc, Tkvall, TkvlocB, TkvallB, Tglup, Teloc, Teall, Tmodall, Tder, Tlam, Tconst] + Twall + Twloc
        S.finish("sp", allT)
        S.finish("pool", allT)
        S.emit(blk)
    return nc


def _tile_tokens(c, xp, xs):
    parts = [xp[0, c * 2048:(c + 1) * 2048], xp[1, c * 2048:(c + 1) * 2048]]
    for j in range(4):
        parts.append(xs[j, c * 512:(c + 1) * 512])
    return np.concatenate(parts, axis=0)


def kernel(**inp):
    import time as _time, sys as _sys
    _t0 = _time.time()
    def _log(msg):
        print("[kernel] %s t=%.1f" % (msg, _time.time() - _t0), file=_sys.stderr, flush=True)
    f32 = np.float32
    g = {k: np.asarray(v) for k, v in inp.items()}
    xp, xs = g["x_prompt"].astype(f32, copy=False), g["x_sample"].astype(f32, copy=False)
    wsl = np.zeros((2, SLAB_PER_LAYER, 128, 2048), f32)
    for l in range(2):
        for i, (name, rows, cols) in enumerate(PLAN):
            W = g[name][l]
            m = len(cols)
            blk = W[:, cols].reshape(-1, 128, m)[rows]
            if m < 128:
                blk = np.concatenate([blk, np.zeros((len(rows), 128, 128 - m), f32)], axis=2)
            wsl[l, i, :, :len(rows) * 128] = blk.transpose(1, 0, 2).reshape(128, -1)
    wada = g["w_ada"].astype(f32, copy=False)
    cT = np.concatenate([g["c_prompt"], g["c_sample"]], 0).astype(f32)
    cT = np.ascontiguousarray(cT.T.reshape(KD, 128, NSEQ).transpose(1, 0, 2))
    bada = g["b_ada"].astype(f32).reshape(2, 144, 128).transpose(2, 0, 1)
    bada = np.ascontiguousarray(np.repeat(bada[:, :, :, None], NSEQ, axis=3))
    gn = g["g_norm"].astype(f32).reshape(2, 4, KD, 128).transpose(3, 0, 1, 2)
    gn6 = np.ascontiguousarray(np.repeat(gn[..., None], NSEQ, axis=4))
    gn3 = np.ascontiguousarray(gn[:, :, 3, :])

    def p64(v):
        v = v.reshape(-1, 64)
        return np.concatenate([np.concatenate([b[32:], b[:32]]) for b in v])
    vecs = np.zeros((128, 2, 32), f32)
    for l in range(2):
        gq, gk = g["g_mla_q"][l], g["g_mla_k"][l]
        vecs[:, l, 0] = gq[:128]; vecs[:64, l, 1] = gq[128:]; vecs[:64, l, 2] = p64(gq[128:])
        vecs[:, l, 3] = gk[:128]; vecs[:64, l, 4] = gk[128:]; vecs[:64, l, 5] = p64(gk[128:])
        dq2 = np.concatenate([g["g_diff_q"][l]] * 2); dk2 = np.concatenate([g["g_diff_k"][l]] * 2)
        vecs[:, l, 6] = dq2; vecs[:, l, 7] = p64(dq2); vecs[:, l, 8] = dk2; vecs[:, l, 9] = p64(dk2)
        vecs[:, l, 12] = g["g_diff_sub"][l]
        vecs[:, l, 16:20] = g["g_cq"][l].reshape(4, 128).T
        vecs[:, l, 20:22] = g["g_ckv"][l].reshape(2, 128).T
    lam = np.stack([np.stack([g[k][l] for k in ("lam_q1", "lam_k1", "lam_q2", "lam_k2")], -1) for l in range(2)], 1).astype(f32)
    wdw = np.ascontiguousarray(g["w_dw"].astype(f32).reshape(2, 31, 4, 128).transpose(3, 0, 2, 1))
    cvec = np.ascontiguousarray(np.stack([g[k].astype(f32).reshape(2, 4, 128).transpose(2, 0, 1) for k in ("b_dw", "g_conv", "b_conv")], -1))
    ident = np.eye(128, dtype=f32)
    inv = (np.float32(10000.0) ** (-np.arange(32, dtype=f32) * np.float32(2.0 / 64))).astype(f32)
    dimf = np.concatenate([inv, inv, inv, inv])
    sign = np.where((np.arange(128) % 64) < 32, -1.0, 1.0).astype(f32)

    in_maps = []
    for c in range(NCORES):
        xt = np.ascontiguousarray(_tile_tokens(c, xp, xs).T)
        pos = np.zeros((NT, TT), f32)
        for t in range(NT):
            base = c * 2048 + (t % 4) * TT if t < 8 else c * 512
            pos[t] = base + np.arange(TT)
        ang = pos[:, None, :] * dimf[None, :, None]
        cs = np.stack([np.cos(ang), np.sin(ang) * sign[None, :, None]], 2).astype(f32)
        msk = np.zeros((128, 16), f32)
        if c > 0:
            msk[:, c - 1] = 1.0
        if c < NCORES - 1:
            msk[:, 8 + c + 1] = 1.0
        wshard = np.concatenate([wsl[0, c * SLAB_PER_RANK:(c + 1) * SLAB_PER_RANK],
                                 wsl[1, c * SLAB_PER_RANK:(c + 1) * SLAB_PER_RANK]], 0).reshape(-1, 2048)
        wa = []
        for l in range(2):
            for j in range(OC_PER_RANK):
                oc = c * OC_PER_RANK + j
                blk = wada[l][:, oc * 128:(oc + 1) * 128].reshape(KD, 128, 128).transpose(1, 0, 2).reshape(128, -1)
                wa.append(blk)
        in_maps.append({
            "xT": xt, "cT": cT, "wsl": np.ascontiguousarray(wshard), "wada": np.ascontiguousarray(np.concatenate(wa, 0)),
            "bada": bada, "gn6": gn6, "gn3": gn3, "vecs": vecs, "lam": lam, "wdw": wdw, "cvec": cvec,
            "cs": np.ascontiguousarray(cs), "msk": msk, "ident": ident,
        })
    _log("host prep done")
    if CFG.get("synth"):
        for m_ in in_maps:
            for k_ in ("xT", "wsl", "wada"):
                m_.pop(k_)
    nc = build_program()
    _log("program built")
    res = run_bass_kernel_spmd(nc, in_maps, core_ids=list(range(NCORES)))
    _log("run done")
    yp = np.zeros((2, 16384, D), f32)
    ys = np.zeros((4, 4096, D), f32)
    if CFG.get("synth"):
        return (yp, ys)
    for c in range(NCORES):
        y = np.asarray(res.results[c]["yT"]).T
        yp[0, c * 2048:(c + 1) * 2048] = y[0:2048]
        yp[1, c * 2048:(c + 1) * 2048] = y[2048:4096]
        for j in range(4):
            ys[j, c * 512:(c + 1) * 512] = y[4096 + j * 512:4096 + (j + 1) * 512]
    return (yp, ys)
```
